# Optimizing a Trainium2 kernel written in Bass

```python
import jax, jax.numpy as jnp
from jax import lax
import numpy as np


D_MODEL = 1024
BATCH = 1
SEQ = 16384
DEPTH = 2
DEC_BATCH = 8
DEC_SEQ = 2048
PAST_LEN = 128

N_MIXERS = 2
N_A_LAYERS = (DEPTH + 1) // 2
N_B_LAYERS = DEPTH // 2
D_RNN = D_MODEL
LRU_HEADS = 8
LRU_BLOCK = D_RNN // LRU_HEADS
LRU_CONV = 4
LRU_C = 8.0
ATTN_HEADS = 16
HEAD_DIM = D_MODEL // ATTN_HEADS
GRID_W = 64
WIN_ROWS = 8
WIN_COLS = 16
D_FF = 2816
FFN_CONV = 3
NORM_EPS = 1e-6

kernel_name = 'hybrid_rglru_natten_encoder'


def rms_norm(x, g):
    xf = x.astype(jnp.float32)
    y = xf * lax.rsqrt(jnp.mean(xf * xf, axis=-1, keepdims=True) + NORM_EPS)
    return (y * g.astype(jnp.float32)).astype(x.dtype)


def depthwise_conv(x, w, b, pad_left, pad_right):
    c = x.shape[-1]
    y = lax.conv_general_dilated(
        x, w[:, None, :].astype(x.dtype), window_strides=(1,),
        padding=[(pad_left, pad_right)],
        dimension_numbers=('NWC', 'WIO', 'NWC'), feature_group_count=c)
    return y + b.astype(x.dtype)


def _lin_rec_combine(c1, c2):
    a1, b1 = c1
    a2, b2 = c2
    return a1 * a2, a2 * b1 + b2


def rg_lru_direction(xc, gate_w, gate_b, lam, reverse):
    bsz, s, _ = xc.shape
    xh = xc.reshape(bsz, s, LRU_HEADS, LRU_BLOCK)
    gates = jnp.einsum('bshi,ghij->gbshj', xh, gate_w.astype(xc.dtype)) + gate_b[:, None, None].astype(xc.dtype)
    gates = jax.nn.sigmoid(gates.astype(jnp.float32)).reshape(2, bsz, s, D_RNN)
    r, i = gates[0], gates[1]
    log_a = -LRU_C * r * jax.nn.softplus(-lam.astype(jnp.float32))
    a = jnp.exp(log_a)
    mult = jnp.sqrt(-jnp.expm1(2.0 * log_a))
    b = mult * i * xc.astype(jnp.float32)
    _, h = lax.associative_scan(_lin_rec_combine, (a, b), reverse=reverse, axis=1)
    return h


def rglru_mixer(x, w_in, conv_w, conv_b, gate_w, gate_b, lam, w_out):
    proj = x @ w_in.astype(x.dtype)
    gate_branch, rec_branch = jnp.split(proj, 2, axis=-1)
    xc = depthwise_conv(rec_branch, conv_w, conv_b, LRU_CONV // 2, LRU_CONV - 1 - LRU_CONV // 2)
    h = (rg_lru_direction(xc, gate_w[0], gate_b[0], lam[0], False)
         + rg_lru_direction(xc, gate_w[1], gate_b[1], lam[1], True))
    y = h.astype(x.dtype) * jax.nn.gelu(gate_branch, approximate=True)
    return y @ w_out.astype(x.dtype)


def neighbourhood_attention(x, w_qkv, rpb, w_o):
    bsz, s, _ = x.shape
    rows = s // GRID_W
    kr = min(WIN_ROWS, rows)
    qkv = (x @ w_qkv.astype(x.dtype)).reshape(bsz, rows, GRID_W, 3, ATTN_HEADS, HEAD_DIM)
    q = qkv[:, :, :, 0] * (HEAD_DIM ** -0.5)
    k = qkv[:, :, :, 1]
    v = qkv[:, :, :, 2]
    row_ids = jnp.arange(rows, dtype=jnp.int32)
    row_start = jnp.clip(row_ids - kr // 2, 0, rows - kr)
    col_ids = jnp.arange(GRID_W, dtype=jnp.int32)
    col_start = jnp.clip(col_ids - WIN_COLS // 2, 0, GRID_W - WIN_COLS)
    col_idx = col_start[:, None] + jnp.arange(WIN_COLS, dtype=jnp.int32)
    dc = col_idx - col_ids[:, None] + (WIN_COLS - 1)

    def one_row(args):
        q_r, r, rs = args
        k_rows = lax.dynamic_slice_in_dim(k, rs, kr, axis=1)
        v_rows = lax.dynamic_slice_in_dim(v, rs, kr, axis=1)
        k_win = k_rows[:, :, col_idx]
        v_win = v_rows[:, :, col_idx]
        dr = rs + jnp.arange(kr, dtype=jnp.int32) - r + (WIN_ROWS - 1)
        bias = rpb[:, dr[None, :, None], dc[:, None, :]]
        scores = (jnp.einsum('bqhd,brqchd->bhqrc', q_r, k_win).astype(jnp.float32)
                  + bias.astype(jnp.float32)[None])
        p = jax.nn.softmax(scores.reshape(bsz, ATTN_HEADS, GRID_W, kr * WIN_COLS), axis=-1)
        p = p.reshape(scores.shape).astype(v.dtype)
        return jnp.einsum('bhqrc,brqchd->bqhd', p, v_win)

    out = lax.map(one_row, (jnp.moveaxis(q, 1, 0), row_ids, row_start))
    out = jnp.moveaxis(out, 0, 1).reshape(bsz, s, D_MODEL)
    return out @ w_o.astype(x.dtype)


def conv_ffn(x, w_up, conv_w, conv_b, w_down):
    h = x @ w_up.astype(x.dtype)
    h = depthwise_conv(h, conv_w, conv_b, FFN_CONV // 2, FFN_CONV // 2)
    g, u = jnp.split(h, 2, axis=-1)
    return (jax.nn.gelu(g, approximate=True) * u) @ w_down.astype(x.dtype)


def trunk(x, norm_mix, norm_ffn, norm_final, lru_w_in, lru_conv_w, lru_conv_b,
          lru_gate_w, lru_gate_b, lru_lambda, lru_w_out, attn_w_qkv, attn_rpb,
          attn_w_o, ffn_w_up, ffn_conv_w, ffn_conv_b, ffn_w_down):
    for i in range(DEPTH):
        j = i // N_MIXERS
        h = rms_norm(x, norm_mix[i])
        if i % N_MIXERS == 0:
            h = rglru_mixer(h, lru_w_in[j], lru_conv_w[j], lru_conv_b[j],
                            lru_gate_w[j], lru_gate_b[j], lru_lambda[j], lru_w_out[j])
        else:
            h = neighbourhood_attention(h, attn_w_qkv[j], attn_rpb[j], attn_w_o[j])
        x = x + h
        x = x + conv_ffn(rms_norm(x, norm_ffn[i]), ffn_w_up[i], ffn_conv_w[i],
                         ffn_conv_b[i], ffn_w_down[i])
    return rms_norm(x, norm_final)


def setup_inputs(seed: int = 0) -> dict:
    key = jax.random.key(seed)
    ks = jax.random.split(key, 20)

    def nrm(k, shape, scale):
        return jax.random.normal(k, shape, jnp.float32) * scale

    x_prompt = nrm(ks[0], (BATCH, SEQ, D_MODEL), 1.0)
    x_sample = nrm(ks[1], (DEC_BATCH, DEC_SEQ, D_MODEL), 1.0)
    norm_mix = 1.0 + nrm(ks[2], (DEPTH, D_MODEL), 0.02)
    norm_ffn = 1.0 + nrm(ks[3], (DEPTH, D_MODEL), 0.02)
    norm_final = 1.0 + nrm(ks[4], (D_MODEL,), 0.02)
    lru_w_in = nrm(ks[5], (N_A_LAYERS, D_MODEL, 2 * D_RNN), D_MODEL ** -0.5)
    lru_conv_w = nrm(ks[6], (N_A_LAYERS, LRU_CONV, D_RNN), LRU_CONV ** -0.5)
    lru_conv_b = nrm(ks[7], (N_A_LAYERS, D_RNN), 0.01)
    lru_gate_w = nrm(ks[8], (N_A_LAYERS, 2, 2, LRU_HEADS, LRU_BLOCK, LRU_BLOCK), LRU_BLOCK ** -0.5)
    lru_gate_b = nrm(ks[9], (N_A_LAYERS, 2, 2, LRU_HEADS, LRU_BLOCK), 0.01)
    a_target = jax.random.uniform(ks[10], (N_A_LAYERS, 2, D_RNN), jnp.float32, 0.9, 0.999)
    s = a_target ** (1.0 / LRU_C)
    lru_lambda = jnp.log(s) - jnp.log1p(-s)
    lru_w_out = nrm(ks[11], (N_A_LAYERS, D_RNN, D_MODEL), D_RNN ** -0.5)
    attn_w_qkv = nrm(ks[12], (N_B_LAYERS, D_MODEL, 3 * D_MODEL), D_MODEL ** -0.5)
    attn_rpb = nrm(ks[13], (N_B_LAYERS, ATTN_HEADS, 2 * WIN_ROWS - 1, 2 * WIN_COLS - 1), 0.1)
    attn_w_o = nrm(ks[14], (N_B_LAYERS, D_MODEL, D_MODEL), D_MODEL ** -0.5)
    ffn_w_up = nrm(ks[15], (DEPTH, D_MODEL, 2 * D_FF), D_MODEL ** -0.5)
    ffn_conv_w = nrm(ks[16], (DEPTH, FFN_CONV, 2 * D_FF), FFN_CONV ** -0.5)
    ffn_conv_b = nrm(ks[17], (DEPTH, 2 * D_FF), 0.01)
    ffn_w_down = nrm(ks[18], (DEPTH, D_FF, D_MODEL), D_FF ** -0.5)
    return {'x_prompt': x_prompt, 'x_sample': x_sample, 'norm_mix': norm_mix,
            'norm_ffn': norm_ffn, 'norm_final': norm_final, 'lru_w_in': lru_w_in,
            'lru_conv_w': lru_conv_w, 'lru_conv_b': lru_conv_b, 'lru_gate_w': lru_gate_w,
            'lru_gate_b': lru_gate_b, 'lru_lambda': lru_lambda, 'lru_w_out': lru_w_out,
            'attn_w_qkv': attn_w_qkv, 'attn_rpb': attn_rpb, 'attn_w_o': attn_w_o,
            'ffn_w_up': ffn_w_up, 'ffn_conv_w': ffn_conv_w, 'ffn_conv_b': ffn_conv_b,
            'ffn_w_down': ffn_w_down}


def reference(x_prompt, x_sample, norm_mix, norm_ffn, norm_final, lru_w_in, lru_conv_w,
              lru_conv_b, lru_gate_w, lru_gate_b, lru_lambda, lru_w_out, attn_w_qkv,
              attn_rpb, attn_w_o, ffn_w_up, ffn_conv_w, ffn_conv_b, ffn_w_down):
    y_prompt = trunk(x_prompt, norm_mix, norm_ffn, norm_final, lru_w_in, lru_conv_w,
                     lru_conv_b, lru_gate_w, lru_gate_b, lru_lambda, lru_w_out,
                     attn_w_qkv, attn_rpb, attn_w_o, ffn_w_up, ffn_conv_w, ffn_conv_b,
                     ffn_w_down)
    y_sample = trunk(x_sample, norm_mix, norm_ffn, norm_final, lru_w_in, lru_conv_w,
                     lru_conv_b, lru_gate_w, lru_gate_b, lru_lambda, lru_w_out,
                     attn_w_qkv, attn_rpb, attn_w_o, ffn_w_up, ffn_conv_w, ffn_conv_b,
                     ffn_w_down)
    return (y_prompt, y_sample)
```

```python
import contextlib
import numpy as np
import concourse.bass as bass
import concourse.mybir as mybir
from concourse.bass_utils import run_bass_kernel_spmd

F32 = mybir.dt.float32
BF16 = mybir.dt.bfloat16
AF = mybir.ActivationFunctionType
ALU = mybir.AluOpType

NCORES = 8
D = 1024
T = 2048
NCH = 8
DFF = 2816
NFC = 22
HL = 4
TP = T + 2 * HL
TK = 2560
EPS = 1e-6
NEG = -30000.0
SEM_WRAP = 30000
GROUPS = [(0, 6), (6, 12), (12, 17), (17, 22)]

PV_G = 0
PV_LCW = 40
PV_LCB = 72
PV_LGB = 80
PV_LAM = 112
PV_FC = 128
PV_FLAG = 480
PV_MF = 484
PV_MB = 492
PV_SL = 500
PV_SR = 508
NPV = 520


class Res:
    __slots__ = ("name", "writers", "readers")

    def __init__(self, name, writers=()):
        self.name = name
        self.writers = list(writers)
        self.readers = []


class Op:
    __slots__ = ("eng", "fn", "deps", "signal", "sig", "dma", "semkey", "dmaval")

    def __init__(self, eng, fn):
        self.eng = eng
        self.fn = fn
        self.deps = ()
        self.signal = False
        self.sig = None
        self.dma = False
        self.semkey = None
        self.dmaval = 0


class _Rec:
    def __init__(self):
        self.call = None

    def __getattr__(self, name):
        def f(*a, **k):
            self.call = (name, a, k)
            return self
        return f


class Sched:
    ENGS = ("pe", "act", "dve", "pool", "sp")

    def __init__(self, nc):
        self.nc = nc
        self.ops = {e: [] for e in self.ENGS}
        self.dma_cnt = {}
        self.dma_last = {}

    def add(self, eng, fn, reads=(), writes=(), dma=None, group=False, cc=False):
        rec = _Rec()
        fn(rec)
        op = Op(eng, rec.call)
        deps = set()
        for r in reads:
            deps.update(r.writers)
        for w in writes:
            deps.update(w.writers)
            deps.update(w.readers)
        if dma is not None:
            op.dma = True
            op.semkey = dma
            if not group:
                deps.update(self.dma_last.get(dma, []))
                self.dma_last[dma] = []
            else:
                deps.difference_update(self.dma_last.get(dma, []))
            self.dma_cnt[dma] = self.dma_cnt.get(dma, 0) + (1 if cc else 16)
            op.signal = cc
            self.dma_last.setdefault(dma, []).append(op)
            for g_op in self.dma_last[dma]:
                g_op.dmaval = self.dma_cnt[dma]
        deps.discard(op)
        best = {}
        dd = []
        for d in deps:
            if d.dma:
                dd.append(d)
            else:
                b = best.get(d.eng)
                if b is None or d.sig > b.sig:
                    best[d.eng] = d
        op.deps = dd + list(best.values())
        op.sig = len(self.ops[eng])
        for r in reads:
            r.readers.append(op)
        for w in writes:
            w.writers = [op]
            w.readers = []
        self.ops[eng].append(op)
        return op

    def emit(self):
        nc = self.nc
        for e in self.ENGS:
            for op in self.ops[e]:
                for d in op.deps:
                    if d.dma:
                        continue
                    if d.eng == "pe" and op.eng == "pe" and not op.dma:
                        continue
                    d.signal = True
        with contextlib.ExitStack() as st:
            esems = {}
            for e in self.ENGS:
                cnt = 0
                for o in self.ops[e]:
                    if o.signal and not o.dma:
                        o.sig = (e, cnt // SEM_WRAP, cnt % SEM_WRAP + 1)
                        cnt += 1
                    else:
                        o.sig = None
                nsem = max(1, (cnt + SEM_WRAP - 1) // SEM_WRAP)
                esems[e] = [st.enter_context(nc.semaphore(f"s_{e}_{i}")) for i in range(nsem)]
            dsems = {k: st.enter_context(nc.semaphore(f"d_{k}")) for k in self.dma_cnt}
            block = st.enter_context(nc.Block())
            decos = {"pe": block.tensor, "act": block.scalar, "dve": block.vector,
                     "pool": block.gpsimd, "sp": block.sync}

            def make(e):
                def body(eng):
                    known = {}
                    for op in self.ops[e]:
                        need = {}
                        for d in op.deps:
                            if d.dma:
                                key = ("d", d.semkey, 0)
                                val = d.dmaval
                            else:
                                if d.eng == "pe" and e == "pe" and not op.dma:
                                    continue
                                key = ("e", d.sig[0], d.sig[1])
                                val = d.sig[2]
                            if need.get(key, 0) < val:
                                need[key] = val
                        for key, val in need.items():
                            if known.get(key, 0) >= val:
                                continue
                            known[key] = val
                            sem = dsems[key[1]] if key[0] == "d" else esems[key[1]][key[2]]
                            eng.wait_ge(sem, val)
                        name, a, k = op.fn
                        ins = getattr(eng, name)(*a, **k)
                        if op.dma and op.signal:
                            ins.then_inc(dsems[op.semkey])
                        elif op.dma:
                            ins.then_inc(dsems[op.semkey], 16)
                        elif op.signal:
                            ins.then_inc(esems[op.sig[0]][op.sig[1]], 1)
                    if e == "sp":
                        for k, v in self.dma_cnt.items():
                            eng.wait_ge(dsems[k], v)
                return body

            for e in self.ENGS:
                if self.ops[e] or e == "sp":
                    decos[e](make(e))


class Arena:
    def __init__(self, nc, base, limit):
        self.nc = nc
        self.base = base
        self.limit = limit
        self.off = base
        self.cnt = 0
        self.res = []
        self.frontier = []

    def res_only(self, name):
        r = Res(name, self.frontier)
        self.res.append(r)
        return r

    def alloc(self, name, shape, dt):
        n = 1
        for s in shape[1:]:
            n *= s
        nbytes = n * (4 if dt == F32 else 2)
        nbytes = (nbytes + 31) // 32 * 32
        h = self.nc.alloc_sbuf_tensor_at(f"{name}_{self.cnt}", list(shape), dt, offset=self.off)
        self.cnt += 1
        self.off += nbytes
        assert self.off <= self.limit, (name, self.off, self.limit)
        return h

    def new_phase(self):
        f = set()
        for r in self.res:
            f.update(r.writers)
            f.update(r.readers)
        self.frontier = list(f)
        self.res = []
        self.peak = max(getattr(self, "peak", 0), self.off)
        self.off = self.base


def build(do_p=True, do_s=True, nlayers=2, ncores=NCORES):
    nc = bass.Bass("TRN2", target_bir_lowering=False)

    def din(name, shape, dt=F32):
        return nc.dram_tensor(name, list(shape), dt, kind="ExternalInput").ap()

    xS = din("xS", [D, T])
    xP = din("xP", [D, T])
    xPh = din("xPh", [D, 4])
    pvec_d = din("pvec", [128, NPV])
    w_in_d = din("w_in_t", [16, 128, 8, 128])
    gw_d = din("gw_t", [8, 128, 4, 128])
    w_out_d = din("w_out_t", [8, 128, 8, 128])
    w_up_d = din("w_up_t", [2, 44, 128, 8, 128])
    w_dn_d = din("w_dn_t", [2, 128, NFC, D])
    w_qkv_d = din("w_qkv_t", [24, 128, 8, 128])
    w_o_d = din("w_o_t", [8, 128, 8, 128])
    biasF_d = din("biasF", [16, 128, 1024])
    maskF_d = din("maskF", [128, 1024])
    maskI_d = din("maskI", [128, 640])
    ident_d = din("ident", [128, 128])
    yS = nc.dram_tensor("yS", [D, T], F32, kind="ExternalOutput").ap()
    yP = nc.dram_tensor("yP", [D, T], F32, kind="ExternalOutput").ap()
    cc_in = [nc.dram_tensor(f"cc_in{i}", [128, 32], F32) for i in range(3)]
    cc_out = [nc.dram_tensor(f"cc_out{i}", [ncores * 128, 32], F32) for i in range(3)]
    cc3_in = nc.dram_tensor("cc3_in", [128, 8 * 512], BF16)
    cc3_out = nc.dram_tensor("cc3_out", [ncores * 128, 8 * 512], BF16)

    S = Sched(nc)
    st = contextlib.ExitStack()

    off = [16512]

    def palloc(name, shape, dt):
        n = 1
        for s in shape[1:]:
            n *= s
        nb = (n * (4 if dt == F32 else 2) + 31) // 32 * 32
        h = nc.alloc_sbuf_tensor_at(name, list(shape), dt, offset=off[0])
        off[0] += nb
        return h

    X = palloc("X", [128, NCH, T], F32)
    PV = palloc("PV", [128, NPV], F32)
    NSP = palloc("NSP", [128, 16], F32)
    NSPt = palloc("NSPt", [128, 16], F32)
    ONES = palloc("ONES", [128, 128], BF16)
    IDENT = palloc("IDENT", [128, 128], BF16)
    ZERO = palloc("ZERO", [128, 1024], F32)
    CARRY = palloc("CARRY", [128, 2, 8], F32)
    SBUF_LIMIT = 229344 - 64
    AR = Arena(nc, off[0], SBUF_LIMIT)

    r_X = [[Res(f"X{c}_{tt}") for tt in range(4)] for c in range(NCH)]
    r_PV, r_NSP, r_ONES, r_ID, r_ZERO, r_CARRY = Res("PV"), Res("NSP"), Res("ONES"), Res("ID"), Res("ZERO"), Res("CARRY")

    PSF = st.enter_context(nc.psum_tensor("PSF", [128, 7, 512], F32))
    PSB = st.enter_context(nc.psum_tensor("PSB", [128, 1024], BF16))
    r_ps = [Res(f"ps{i}") for i in range(7)]
    r_psb = Res("psb")
    bank_i = [0]

    bank_lim = [7]

    def bank():
        b = bank_i[0] % bank_lim[0]
        bank_i[0] += 1
        return b

    def mm(out_ap, pairs, reads, out_res):
        n = len(pairs)
        for i, (l, r) in enumerate(pairs):
            S.add("pe", lambda e, l=l, r=r, i=i: e.matmul(out_ap, lhsT=l, rhs=r, start=(i == 0), stop=(i == n - 1)),
                  reads=reads, writes=[out_res])

    S.add("sp", lambda e: e.dma_start(out=PV[:], in_=pvec_d[:, :]), writes=[r_PV], dma="pv")
    S.add("pool", lambda e: e.dma_start(out=IDENT[:], in_=ident_d[:, :]), writes=[r_ID], dma="id")
    S.add("dve", lambda e: e.memset(ONES[:], 1.0), writes=[r_ONES])
    S.add("dve", lambda e: e.memset(ZERO[:], 0.0), writes=[r_ZERO])
    S.add("dve", lambda e: e.memset(CARRY[:], 0.0), writes=[r_CARRY])
    S.add("act", lambda e: e.activation(out=NSP[:], in_=PV[:, PV_LAM:PV_LAM + 16], func=AF.Exp, scale=-1.0),
          reads=[r_PV], writes=[r_NSP])
    S.add("act", lambda e: e.activation(out=NSP[:], in_=NSP[:], func=AF.Ln, bias=PV[:, NPV - 2:NPV - 1], scale=1.0),
          reads=[r_NSP, r_PV], writes=[r_NSP])
    S.add("dve", lambda e: e.tensor_scalar(out=NSP[:], in0=NSP[:], scalar1=-8.0, scalar2=None, op0=ALU.mult),
          reads=[r_NSP], writes=[r_NSP])

    def pv(col):
        return PV[:, col:col + 1]

    def load_x(src, seg):
        for c in range(NCH):
            S.add("sp", lambda e, c=c: e.dma_start(out=X[:, c, :], in_=src[c * 128:(c + 1) * 128, :]),
                  writes=r_X[c], dma="x", group=(c > 0))

    def norm_tile(src_ap, src_res, n, gcol, dst_ap, dst_res, SQ, r_SQ, STD, r_STD):
        b = bank()
        for c in range(NCH):
            S.add("act", lambda e, c=c: e.activation(out=SQ[:, c, 0:n], in_=src_ap(c), func=AF.Square),
                  reads=src_res(c), writes=[r_SQ[c]])
        mm(PSF[:, b, 0:n], [(ONES[:], SQ[:, c, 0:n]) for c in range(NCH)], [r_ONES] + r_SQ, r_ps[b])
        S.add("act", lambda e: e.activation(out=STD[:, 0:n], in_=PSF[:, b, 0:n], func=AF.Sqrt, scale=1.0 / D, bias=PV[:, NPV - 1:NPV]),
              reads=[r_ps[b], r_PV], writes=[r_STD])
        S.add("dve", lambda e: e.reciprocal(out=STD[:, 0:n], in_=STD[:, 0:n]), reads=[r_STD], writes=[r_STD])
        for c in range(NCH):
            S.add("dve", lambda e, c=c: e.scalar_tensor_tensor(out=dst_ap(c), in0=src_ap(c), scalar=pv(gcol + c),
                                                               in1=STD[:, 0:n], op0=ALU.mult, op1=ALU.mult),
                  reads=src_res(c) + [r_STD, r_PV], writes=dst_res(c))

    def norm_scratch():
        SQ = AR.alloc("SQ", [128, NCH, 512], BF16)
        STD = AR.alloc("STD", [128, 512], F32)
        return SQ, [AR.res_only(f"SQ{c}") for c in range(NCH)], STD, AR.res_only("STD")

    def norm_main(gcol, XN, r_XN, scr, col0=HL):
        SQ, r_SQ, STD, r_STD = scr
        for tt in range(4):
            norm_tile(lambda c: X[:, c, tt * 512:(tt + 1) * 512], lambda c: [r_X[c][tt]], 512, gcol,
                      lambda c: XN[:, c, col0 + tt * 512: col0 + (tt + 1) * 512], lambda c: [r_XN[c][tt]],
                      SQ, r_SQ, STD, r_STD)

    def allgather(idx, src_sb_ap, r_src, dst_sb, r_dst, tag):
        r_ci, r_co = Res("ci"), Res("co")
        S.add("pool", lambda e: e.dma_start(out=cc_in[idx].ap(), in_=src_sb_ap), reads=r_src, writes=[r_ci], dma=f"cci{idx}")
        S.add("pool", lambda e: e.collective_compute("AllGather", ALU.bypass, replica_groups=[list(range(ncores))],
                                                     ins=[cc_in[idx].ap()], outs=[cc_out[idx].ap()]),
              reads=[r_ci], writes=[r_co], dma=f"ccx{idx}", cc=True)
        S.add("pool", lambda e: e.dma_start(out=dst_sb[:], in_=cc_out[idx].ap().rearrange("(r p) n -> p r n", p=128)),
              reads=[r_co], writes=r_dst, dma=f"cco{idx}")

    def mask_sum(out_ap, r_out, src_ap_of_rank, r_srcs, selcol):
        for r in range(ncores):
            if r == 0:
                S.add("dve", lambda e, r=r: e.tensor_scalar(out=out_ap, in0=src_ap_of_rank(r), scalar1=pv(selcol + r), scalar2=None, op0=ALU.mult),
                      reads=r_srcs + [r_PV], writes=r_out)
            else:
                S.add("dve", lambda e, r=r: e.scalar_tensor_tensor(out=out_ap, in0=src_ap_of_rank(r), scalar=pv(selcol + r), in1=out_ap,
                                                                   op0=ALU.mult, op1=ALU.add),
                      reads=r_srcs + [r_PV] + r_out, writes=r_out)

    def mixer_lru(is_p):
        AR.new_phase()
        XN = AR.alloc("XN", [128, NCH, TP], BF16)
        r_XN = [[AR.res_only(f"XN{c}_{tt}") for tt in range(4)] for c in range(NCH)]
        r_XNh = AR.res_only("XNh")
        scr = norm_scratch()
        Y = AR.alloc("Y", [128, 4, T], BF16)
        r_Y = [AR.res_only(f"Y{c}") for c in range(4)]
        REC = AR.alloc("REC", [128, T + 8], F32)
        r_REC = AR.res_only("REC")
        XC = AR.alloc("XC", [128, T], F32)
        r_XC = AR.res_only("XC")
        XCb = AR.alloc("XCb", [128, T], BF16)
        r_XCb = AR.res_only("XCb")
        GG = AR.alloc("GG", [128, T], BF16)
        r_GG = AR.res_only("GG")
        UU = [[AR.alloc(f"U{d}{i}", [128, 1024], F32) for i in range(3)] for d in range(2)]
        r_UU = [[AR.res_only(f"U{d}{i}") for i in range(3)] for d in range(2)]
        Hd = [AR.alloc(f"H{i}", [128, T], F32) for i in range(2)]
        r_H = [AR.res_only(f"H{i}") for i in range(2)]
        WI = [AR.alloc(f"WI{i}", [128, 2, 8, 128], BF16) for i in range(2)]
        r_WI = [[AR.res_only(f"WI{i}_{w}") for w in range(2)] for i in range(2)]
        GW = [AR.alloc(f"GW{i}", [128, 4, 128], BF16) for i in range(2)]
        r_GW = [AR.res_only(f"GW{i}") for i in range(2)]
        WO = [AR.alloc("WO0", [128, 8, 128], BF16)] * 2
        r_WO = [AR.res_only("WO0")] * 2
        STAT = AR.alloc("STAT", [128, 8, 4], F32)
        r_STAT = AR.res_only("STAT")
        GATH = AR.alloc("GATH", [128, ncores, 32], F32)
        r_GATH = AR.res_only("GATH")
        XH = AR.alloc("XH", [128, NCH, 4], F32)
        r_XH = AR.res_only("XH")
        TMP = AR.alloc("TMPc", [128, 2, 8], F32)
        r_TMP = AR.res_only("TMPc")

        norm_main(PV_G + 0, XN, r_XN, scr)
        S.add("pool", lambda e: e.memset(XN[:, :, 0:HL], 0.0), writes=[r_XNh])
        S.add("pool", lambda e: e.memset(XN[:, :, HL + T:TP], 0.0), writes=[r_XNh])
        if is_p:
            S.add("sp", lambda e: e.dma_start(out=XH[:], in_=xPh.rearrange("(c p) n -> p c n", p=128)), writes=[r_XH], dma="xh")
            SQ, r_SQ, STD, r_STD = scr
            norm_tile(lambda c: XH[:, c, 0:2], lambda c: [r_XH], 2, PV_G + 0,
                      lambda c: XN[:, c, 2:4], lambda c: [r_XNh], SQ, r_SQ, STD, r_STD)
            norm_tile(lambda c: XH[:, c, 2:4], lambda c: [r_XH], 2, PV_G + 0,
                      lambda c: XN[:, c, HL + T:HL + T + 2], lambda c: [r_XNh], SQ, r_SQ, STD, r_STD)
        all_xn = [r for c in range(NCH) for r in r_XN[c]] + [r_XNh]

        def load_w(c, wslot):
            S.add("pool", lambda e: e.dma_start(out=WI[wslot][:, 1], in_=w_in_d[8 + c]), writes=[r_WI[wslot][1]], dma=f"wi{wslot}")
            S.add("pool", lambda e: e.dma_start(out=GW[wslot][:], in_=gw_d[c]), writes=[r_GW[wslot]], dma=f"gw{wslot}")
            S.add("pool", lambda e: e.dma_start(out=WI[wslot][:, 0], in_=w_in_d[c]), writes=[r_WI[wslot][0]], dma=f"wi{wslot}", group=True)

        def chunk(c, stats_only, wslot, nxt):
            WIc, GWc = WI[wslot], GW[wslot]
            if nxt is not None:
                load_w(nxt, 1 - wslot)
            segs = [(0, 512), (512, 512), (1024, 512), (1536, 512), (2048, 3)]
            for (s0, n) in segs:
                b = bank()
                mm(PSF[:, b, 0:n], [(WIc[:, 1, k, :], XN[:, k, 2 + s0:2 + s0 + n]) for k in range(NCH)],
                   all_xn + [r_WI[wslot][1]], r_ps[b])
                S.add("act", lambda e, b=b, s0=s0, n=n: e.activation(out=REC[:, s0:s0 + n], in_=PSF[:, b, 0:n], func=AF.Copy),
                      reads=[r_ps[b]], writes=[r_REC])
            S.add("act", lambda e: e.activation(out=XC[:], in_=REC[:, 0:T], func=AF.Identity, scale=pv(PV_LCW + c), bias=pv(PV_LCB + c)),
                  reads=[r_REC, r_PV], writes=[r_XC])
            for k in range(1, 4):
                S.add("dve", lambda e, k=k: e.scalar_tensor_tensor(out=XC[:], in0=REC[:, k:k + T], scalar=pv(PV_LCW + 8 * k + c), in1=XC[:],
                                                                   op0=ALU.mult, op1=ALU.add),
                      reads=[r_REC, r_XC, r_PV], writes=[r_XC])
            S.add("dve", lambda e: e.tensor_copy(out=XCb[:], in_=XC[:]), reads=[r_XC], writes=[r_XCb])
            if not stats_only:
                for tt in range(4):
                    b = bank()
                    mm(PSF[:, b, :], [(WIc[:, 0, k, :], XN[:, k, HL + tt * 512:HL + (tt + 1) * 512]) for k in range(NCH)],
                       [r_XN[k][tt] for k in range(NCH)] + [r_WI[wslot][0]], r_ps[b])
                    S.add("act", lambda e, b=b, tt=tt: e.activation(out=GG[:, tt * 512:(tt + 1) * 512], in_=PSF[:, b, :], func=AF.Gelu_apprx_tanh),
                          reads=[r_ps[b]], writes=[r_GG])
            def rv(ap):
                return bass.AP(ap.tensor, ap.offset + 1023, [list(ap.ap[0]), [-1, 1024]])

            def dir_chain(d):
                ops = []
                U, r_U = UU[d], r_UU[d]
                G1, G2, W2 = U[0], U[1], U[2]
                Hs = Hd[d]
                order = [0, 1] if d == 0 else [1, 0]
                for ui, u in enumerate(order):
                    t0 = u * 1024
                    for g, dst, rr in ((0, G1, r_U[0]), (1, G2, r_U[1])):
                        for hh in range(2):
                            def gate(g=g, dst=dst, rr=rr, hh=hh, t0=t0):
                                b = bank()
                                mm(PSF[:, b, :], [(GWc[:, d * 2 + g, :], XCb[:, t0 + hh * 512:t0 + (hh + 1) * 512])], [r_GW[wslot], r_XCb], r_ps[b])
                                S.add("act", lambda e: e.activation(out=dst[:, hh * 512:(hh + 1) * 512], in_=PSF[:, b, :], func=AF.Sigmoid,
                                                                    bias=pv(PV_LGB + (d * 2 + g) * 8 + c)),
                                      reads=[r_ps[b], r_PV], writes=[rr])
                            ops.append(gate)
                    ops.append(lambda: S.add("act", lambda e: e.activation(out=G1[:], in_=G1[:], func=AF.Exp, scale=NSP[:, d * 8 + c:d * 8 + c + 1]),
                                             reads=[r_U[0], r_NSP], writes=[r_U[0]]))
                    ops.append(lambda: S.add("pool", lambda e: e.tensor_tensor(out=W2[:], in0=G1[:], in1=G1[:], op=ALU.mult), reads=[r_U[0]], writes=[r_U[2]]))
                    ops.append(lambda: S.add("act", lambda e: e.activation(out=W2[:], in_=W2[:], func=AF.Sqrt, scale=-1.0, bias=PV[:, NPV - 2:NPV - 1]),
                                             reads=[r_U[2], r_PV], writes=[r_U[2]]))
                    ops.append(lambda: S.add("pool", lambda e: e.tensor_tensor(out=W2[:], in0=W2[:], in1=G2[:], op=ALU.mult), reads=[r_U[2], r_U[1]], writes=[r_U[2]]))
                    ops.append(lambda t0=t0: S.add("dve", lambda e: e.tensor_tensor(out=W2[:], in0=W2[:], in1=XC[:, t0:t0 + 1024], op=ALU.mult),
                                                   reads=[r_U[2], r_XC], writes=[r_U[2]]))
                    if d == 0:
                        a_ap, b_ap, o_ap = G1[:, :], W2[:, :], Hs[:, t0:t0 + 1024]
                        if ui == 0:
                            init = CARRY[:, 0, c:c + 1] if (is_p and not stats_only) else 0.0
                        else:
                            init = Hs[:, t0 - 1:t0]
                    else:
                        a_ap, b_ap, o_ap = rv(G1[:, :]), rv(W2[:, :]), rv(Hs[:, t0:t0 + 1024])
                        if ui == 0:
                            init = CARRY[:, 1, c:c + 1] if (is_p and not stats_only) else 0.0
                        else:
                            init = Hs[:, t0 + 1024:t0 + 1025]
                    ops.append(lambda a_ap=a_ap, b_ap=b_ap, o_ap=o_ap, init=init: S.add("dve", lambda e: e.tensor_tensor_scan(
                        out=o_ap, data0=a_ap, data1=b_ap, initial=init, op0=ALU.mult, op1=ALU.add),
                        reads=[r_U[0], r_U[2], r_H[d], r_CARRY], writes=[r_H[d]]))
                    if stats_only:
                        if d == 0:
                            cp_o, zz = G2[:, :], ZERO[:, :]
                            cinit = 1.0 if ui == 0 else STAT[:, c, 0:1]
                            last = G2[:, 1023:1024]
                        else:
                            cp_o, zz = rv(G2[:, :]), rv(ZERO[:, :])
                            cinit = 1.0 if ui == 0 else STAT[:, c, 2:3]
                            last = G2[:, 0:1]
                        ops.append(lambda a_ap=a_ap, cp_o=cp_o, zz=zz, cinit=cinit: S.add("dve", lambda e: e.tensor_tensor_scan(
                            out=cp_o, data0=a_ap, data1=zz, initial=cinit, op0=ALU.mult, op1=ALU.add),
                            reads=[r_U[0], r_U[1], r_ZERO, r_STAT], writes=[r_U[1]]))
                        ops.append(lambda last=last: S.add("dve", lambda e: e.tensor_copy(out=STAT[:, c, 2 * d:2 * d + 1], in_=last),
                                                           reads=[r_U[1]], writes=[r_STAT]))
                if stats_only:
                    edge = Hd[d][:, T - 1:T] if d == 0 else Hd[d][:, 0:1]
                    ops.append(lambda: S.add("dve", lambda e: e.tensor_copy(out=STAT[:, c, 2 * d + 1:2 * d + 2], in_=edge),
                                             reads=[r_H[d]], writes=[r_STAT]))
                return ops

            ch0, ch1 = dir_chain(0), dir_chain(1)
            for i in range(max(len(ch0), len(ch1))):
                if i < len(ch0):
                    ch0[i]()
                if i < len(ch1):
                    ch1[i]()
            if not stats_only:
                yc = c % 4
                S.add("pool", lambda e: e.tensor_tensor(out=Hd[0][:], in0=Hd[0][:], in1=Hd[1][:], op=ALU.add),
                      reads=[r_H[0], r_H[1]], writes=[r_H[0]])
                S.add("dve", lambda e: e.tensor_tensor(out=Y[:, yc, :], in0=Hd[0][:], in1=GG[:], op=ALU.mult),
                      reads=[r_H[0], r_GG], writes=[r_Y[yc]])

        def wout_round(rnd):
            for cp in range(NCH):
                ws = cp % 2
                S.add("pool", lambda e, cp=cp, ws=ws: e.dma_start(out=WO[ws][:], in_=w_out_d[cp]), writes=[r_WO[ws]], dma=f"wo{ws}")
                for tt in range(4):
                    b = bank()
                    mm(PSF[:, b, :], [(WO[ws][:, rnd * 4 + k, :], Y[:, k, tt * 512:(tt + 1) * 512]) for k in range(4)],
                       r_Y + [r_WO[ws]], r_ps[b])
                    S.add("dve", lambda e, b=b, cp=cp, tt=tt: e.tensor_tensor(out=X[:, cp, tt * 512:(tt + 1) * 512], in0=X[:, cp, tt * 512:(tt + 1) * 512],
                                                                              in1=PSF[:, b, :], op=ALU.add),
                          reads=[r_ps[b], r_X[cp][tt]], writes=[r_X[cp][tt]])

        seq = ([(c, True) for c in range(NCH)] if is_p else []) + [(c, False) for c in range(NCH)]
        seq_i = [0]

        def run_chunk():
            i = seq_i[0]
            seq_i[0] += 1
            c, so = seq[i]
            nxt = seq[i + 1][0] if i + 1 < len(seq) else None
            chunk(c, so, i % 2, nxt)

        load_w(seq[0][0], 0)
        if is_p:
            for c in range(NCH):
                run_chunk()
            allgather(0, STAT[:].rearrange("p c s -> p (c s)"), [r_STAT], GATH, [r_GATH], "cc1")
            for d in range(2):
                ranks = range(ncores) if d == 0 else range(ncores - 1, -1, -1)
                mcol = PV_MF if d == 0 else PV_MB
                for r in ranks:
                    Ar = GATH[:, r, 2 * d::4]
                    Fr = GATH[:, r, 2 * d + 1::4]
                    S.add("dve", lambda e, Ar=Ar: e.tensor_tensor(out=TMP[:, d, :], in0=Ar, in1=CARRY[:, d, :], op=ALU.mult),
                          reads=[r_GATH, r_CARRY], writes=[r_TMP])
                    S.add("dve", lambda e, Fr=Fr: e.tensor_tensor(out=TMP[:, d, :], in0=TMP[:, d, :], in1=Fr, op=ALU.add),
                          reads=[r_GATH, r_TMP], writes=[r_TMP])
                    S.add("dve", lambda e: e.tensor_tensor(out=TMP[:, d, :], in0=TMP[:, d, :], in1=CARRY[:, d, :], op=ALU.subtract),
                          reads=[r_CARRY, r_TMP], writes=[r_TMP])
                    S.add("dve", lambda e, r=r, mcol=mcol: e.scalar_tensor_tensor(out=CARRY[:, d, :], in0=TMP[:, d, :], scalar=pv(mcol + r), in1=CARRY[:, d, :],
                                                                                  op0=ALU.mult, op1=ALU.add),
                          reads=[r_CARRY, r_TMP, r_PV], writes=[r_CARRY])
        for rnd in range(2):
            for c4 in range(4):
                run_chunk()
            wout_round(rnd)

    def ffn(layer, is_p, cc_idx):
        AR.new_phase()
        XN = AR.alloc("XN", [128, NCH, TP], BF16)
        r_XN = [[AR.res_only(f"XN{c}_{tt}") for tt in range(4)] for c in range(NCH)]
        r_XNh = AR.res_only("XNh")
        scr = norm_scratch()
        HBF = [AR.alloc(f"HBF{w}", [128, T + 2], F32) for w in range(2)]
        r_HBF = [[AR.res_only(f"HBF{w}_{q}") for q in range(2)] for w in range(2)]
        XH2 = AR.alloc("XH2", [128, NCH, 2], BF16)
        r_XH2 = AR.res_only("XH2")
        CVV = [[AR.alloc(f"CV{t}{w}", [128, 1024], F32) for w in range(2)] for t in range(2)]
        r_CVV = [[AR.res_only(f"CV{t}{w}") for w in range(2)] for t in range(2)]
        AG = AR.alloc("AG", [128, 6, T], BF16)
        r_AG = [[AR.res_only(f"AG{f}_{th}") for th in range(2)] for f in range(6)]
        WU = [AR.alloc(f"WU{i}", [128, 2, 8, 128], BF16) for i in range(2)]
        r_WU = [[AR.res_only(f"WU{i}_{w}") for w in range(2)] for i in range(2)]
        WD = [AR.alloc(f"WD{i}", [128, 6, D], BF16) for i in range(2)]
        r_WD = [AR.res_only(f"WD{i}") for i in range(2)]
        BT = AR.alloc("BT", [128, 32], F32)
        r_BT = AR.res_only("BT")
        GATH = AR.alloc("GATH", [128, ncores, 32], F32)
        r_GATH = AR.res_only("GATH")
        XH = AR.alloc("XH", [128, NCH, 2], F32)
        r_XH = AR.res_only("XH")
        gcol = PV_G + 8 + 16 * layer
        fcb = PV_FC + 176 * layer

        if is_p:
            S.add("pool", lambda e: e.memset(BT[:], 0.0), writes=[r_BT])
            S.add("dve", lambda e: e.tensor_copy(out=BT[:, 0:8], in_=X[:, :, 0]), reads=[r_X[c][0] for c in range(NCH)] + [r_BT], writes=[r_BT])
            S.add("dve", lambda e: e.tensor_copy(out=BT[:, 8:16], in_=X[:, :, T - 1]), reads=[r_X[c][3] for c in range(NCH)] + [r_BT], writes=[r_BT])
            allgather(cc_idx, BT[:], [r_BT], GATH, [r_GATH], f"cc{cc_idx}")
        norm_main(gcol, XN, r_XN, scr)
        S.add("pool", lambda e: e.memset(XN[:, :, 0:HL], 0.0), writes=[r_XNh])
        S.add("pool", lambda e: e.memset(XN[:, :, HL + T:TP], 0.0), writes=[r_XNh])
        if is_p:
            mask_sum(XH[:, :, 0], [r_XH], lambda r: GATH[:, r, 8:16], [r_GATH], PV_SL)
            mask_sum(XH[:, :, 1], [r_XH], lambda r: GATH[:, r, 0:8], [r_GATH], PV_SR)
            SQ, r_SQ, STD, r_STD = scr
            norm_tile(lambda c: XH[:, c, 0:1], lambda c: [r_XH], 1, gcol,
                      lambda c: XN[:, c, HL - 1:HL], lambda c: [r_XNh], SQ, r_SQ, STD, r_STD)
            norm_tile(lambda c: XH[:, c, 1:2], lambda c: [r_XH], 1, gcol,
                      lambda c: XN[:, c, HL + T:HL + T + 1], lambda c: [r_XNh], SQ, r_SQ, STD, r_STD)
        all_xn = [r for c in range(NCH) for r in r_XN[c]] + [r_XNh]
        S.add("dve", lambda e: e.tensor_copy(out=XH2[:, :, 0], in_=XN[:, :, HL - 1]), reads=[r_XNh], writes=[r_XH2])
        S.add("dve", lambda e: e.tensor_copy(out=XH2[:, :, 1], in_=XN[:, :, HL + T]), reads=[r_XNh, r_XH2], writes=[r_XH2])

        def load_wu(fc):
            ws = fc % 2
            S.add("pool", lambda e: e.dma_start(out=WU[ws][:, 0], in_=w_up_d[layer, fc]), writes=[r_WU[ws][0]], dma=f"wu{ws}")
            S.add("pool", lambda e: e.dma_start(out=WU[ws][:, 1], in_=w_up_d[layer, NFC + fc]), writes=[r_WU[ws][1]], dma=f"wu{ws}", group=True)

        def load_wd(gi):
            f0, f1 = GROUPS[gi]
            wds = gi % 2
            S.add("pool", lambda e: e.dma_start(out=WD[wds][:, 0:f1 - f0, :], in_=w_dn_d[layer, :, f0:f1, :]),
                  writes=[r_WD[wds]], dma=f"wd{wds}")

        load_wu(0)
        load_wd(0)
        for gi, (f0, f1) in enumerate(GROUPS):
            G = f1 - f0
            wds = gi % 2
            if gi + 1 < len(GROUPS):
                load_wd(gi + 1)
            for fl in range(G):
                fc = f0 + fl
                ws = fc % 2
                if fc + 1 < NFC:
                    load_wu(fc + 1)
                for w in range(2):
                    for tt in range(4):
                        b = bank()
                        mm(PSF[:, b, :], [(WU[ws][:, w, k, :], XN[:, k, HL + tt * 512:HL + (tt + 1) * 512]) for k in range(NCH)],
                           [r_XN[k][tt] for k in range(NCH)] + [r_WU[ws][w]], r_ps[b])
                        S.add("act", lambda e, b=b, w=w, tt=tt: e.activation(out=HBF[w][:, 1 + tt * 512:1 + (tt + 1) * 512], in_=PSF[:, b, :], func=AF.Copy),
                              reads=[r_ps[b]], writes=[r_HBF[w][tt // 2]])
                    b = bank()
                    mm(PSF[:, b, 0:2], [(WU[ws][:, w, k, :], XH2[:, k, :]) for k in range(NCH)], [r_XH2, r_WU[ws][w]], r_ps[b])
                    S.add("act", lambda e, b=b, w=w: e.activation(out=HBF[w][:, 0:1], in_=PSF[:, b, 0:1], func=AF.Copy),
                          reads=[r_ps[b]], writes=[r_HBF[w][0]])
                    S.add("act", lambda e, b=b, w=w: e.activation(out=HBF[w][:, T + 1:T + 2], in_=PSF[:, b, 1:2], func=AF.Copy),
                          reads=[r_ps[b]], writes=[r_HBF[w][1]])

                def th_chain(th, fc=fc, fl=fl, ws=ws):
                    ops = []
                    CV, r_CV = CVV[th], r_CVV[th]
                    o0 = th * 1024
                    for w in range(2):
                        fidx = w * NFC + fc
                        rd = [r_HBF[w][0], r_HBF[w][1]]
                        ops.append(lambda w=w, fidx=fidx, rd=rd: S.add("act", lambda e: e.activation(out=CV[w][:], in_=HBF[w][:, o0 + 1:o0 + 1025], func=AF.Identity,
                                                                                                   scale=pv(fcb + 44 + fidx), bias=pv(fcb + 132 + fidx)),
                                                                       reads=rd + [r_PV], writes=[r_CV[w]]))
                        ops.append(lambda w=w, fidx=fidx, rd=rd: S.add("dve", lambda e: e.scalar_tensor_tensor(out=CV[w][:], in0=HBF[w][:, o0:o0 + 1024], scalar=pv(fcb + fidx), in1=CV[w][:],
                                                                                                               op0=ALU.mult, op1=ALU.add),
                                                                       reads=rd + [r_CV[w], r_PV], writes=[r_CV[w]]))
                        ops.append(lambda w=w, fidx=fidx, rd=rd: S.add("dve", lambda e: e.scalar_tensor_tensor(out=CV[w][:], in0=HBF[w][:, o0 + 2:o0 + 1026], scalar=pv(fcb + 88 + fidx), in1=CV[w][:],
                                                                                                               op0=ALU.mult, op1=ALU.add),
                                                                       reads=rd + [r_CV[w], r_PV], writes=[r_CV[w]]))
                    ops.append(lambda: S.add("act", lambda e: e.activation(out=CV[0][:], in_=CV[0][:], func=AF.Gelu_apprx_tanh), reads=[r_CV[0]], writes=[r_CV[0]]))
                    ops.append(lambda: S.add("pool", lambda e: e.tensor_tensor(out=AG[:, fl, th * 1024:(th + 1) * 1024], in0=CV[0][:], in1=CV[1][:], op=ALU.mult),
                                             reads=[r_CV[0], r_CV[1]], writes=[r_AG[fl][th]]))
                    return ops
                ch0, ch1 = th_chain(0), th_chain(1)
                for i in range(max(len(ch0), len(ch1))):
                    if i < len(ch0):
                        ch0[i]()
                    if i < len(ch1):
                        ch1[i]()
            for cp in range(NCH):
                for tt in range(4):
                    b = bank()
                    mm(PSF[:, b, :], [(WD[wds][:, k, cp * 128:(cp + 1) * 128], AG[:, k, tt * 512:(tt + 1) * 512]) for k in range(G)],
                       [r_AG[k][tt // 2] for k in range(G)] + [r_WD[wds]], r_ps[b])
                    S.add("dve", lambda e, b=b, cp=cp, tt=tt: e.tensor_tensor(out=X[:, cp, tt * 512:(tt + 1) * 512], in0=X[:, cp, tt * 512:(tt + 1) * 512],
                                                                              in1=PSF[:, b, :], op=ALU.add),
                          reads=[r_ps[b], r_X[cp][tt]], writes=[r_X[cp][tt]])

    def attention(is_p):
        AR.new_phase()
        XN = AR.alloc("XNA", [128, NCH, TK], BF16)
        r_XN = [[AR.res_only(f"XN{c}_{tt}") for tt in range(4)] for c in range(NCH)]
        r_XNh = AR.res_only("XNh")
        scr = norm_scratch()
        Q = AR.alloc("Q", [128, T], BF16)
        r_Q = AR.res_only("Q")
        K = AR.alloc("K", [128, TK], BF16)
        r_K = AR.res_only("K")
        VT = AR.alloc("VT", [128, 20, 2, 65], BF16)
        r_VT = AR.res_only("VT")
        OF = AR.alloc("OF", [128, 4, T], BF16)
        r_OF = [AR.res_only(f"OF{c}") for c in range(4)]
        TABI = [AR.alloc(f"TABI{i}", [128, 640], BF16) for i in range(2)]
        TABF = [AR.alloc(f"TABF{i}", [128, 1024], BF16) for i in range(2)]
        r_TAB = [AR.res_only(f"TAB{i}") for i in range(2)]
        BST = AR.alloc("BST", [128, 1024], F32)
        r_BST = AR.res_only("BST")
        MKF = AR.alloc("MKF", [128, 1024], F32)
        MKI = AR.alloc("MKI", [128, 640], F32)
        r_MK = AR.res_only("MK")
        SS = [AR.alloc(f"SS{i}", [128, 640], F32) for i in range(2)]
        r_SS = [AR.res_only(f"SS{i}") for i in range(2)]
        PT = [AR.alloc(f"PT{i}", [128, 640], BF16) for i in range(2)]
        r_PT = [AR.res_only(f"PT{i}") for i in range(2)]
        OT = [AR.alloc(f"OT{i}", [128, 128], BF16) for i in range(2)]
        r_OT = [AR.res_only(f"OT{i}") for i in range(2)]
        RD = AR.alloc("RD", [128, 2], F32)
        r_RD = AR.res_only("RD")
        WQ = [AR.alloc(f"WQ{i}", [128, 3, 8, 128], BF16) for i in range(2)]
        r_WQ = [[AR.res_only(f"WQ{i}_{j}") for j in range(3)] for i in range(2)]
        WO = [AR.alloc(f"WO{i}", [128, 8, 128], BF16) for i in range(2)]
        r_WO = [AR.res_only(f"WO{i}") for i in range(2)]
        GH = AR.alloc("GH", [128, ncores, 512], BF16) if is_p else None
        r_GH = AR.res_only("GH")
        r_c3i, r_c3o = Res("c3i"), Res("c3o")

        S.add("sp", lambda e: e.dma_start(out=MKF[:], in_=maskF_d[:, :]), writes=[r_MK], dma="mkf")
        S.add("sp", lambda e: e.dma_start(out=MKI[:], in_=maskI_d[:, :]), writes=[r_MK], dma="mki")
        norm_main(PV_G + 16, XN, r_XN, scr, col0=256)
        S.add("pool", lambda e: e.memset(XN[:, :, 0:256], 0.0), writes=[r_XNh])
        S.add("pool", lambda e: e.memset(XN[:, :, 256 + T:TK], 0.0), writes=[r_XNh])
        S.add("pool", lambda e: e.memset(VT[:, :, :, 64:65], 1.0), writes=[r_VT])
        if is_p:
            for c in range(NCH):
                S.add("sp", lambda e, c=c: e.dma_start(out=cc3_in.ap()[:, c * 512:c * 512 + 256], in_=XN[:, c, 256:512]),
                      reads=[r_XN[c][0]], writes=[r_c3i], dma="c3i", group=(c > 0))
                S.add("sp", lambda e, c=c: e.dma_start(out=cc3_in.ap()[:, c * 512 + 256:(c + 1) * 512], in_=XN[:, c, 256 + T - 256:256 + T]),
                      reads=[r_XN[c][3]], writes=[r_c3i], dma="c3i", group=True)
            import os
            if not os.environ.get("SKIPCC3"):
                S.add("pool", lambda e: e.collective_compute("AllGather", ALU.bypass, replica_groups=[list(range(ncores))],
                                                             ins=[cc3_in.ap()], outs=[cc3_out.ap()]),
                      reads=[r_c3i], writes=[r_c3o], dma="ccx3", cc=True)
            g3 = cc3_out.ap().rearrange("(r p) n -> p r n", p=128)
            for c in range(NCH):
                S.add("sp", lambda e, c=c: e.dma_start(out=GH[:], in_=g3[:, :, c * 512:(c + 1) * 512]), reads=[r_c3o], writes=[r_GH], dma="gh")
                mask_sum(XN[:, c, 0:256], [r_XNh], lambda r: GH[:, r, 256:512], [r_GH], PV_SL)
                mask_sum(XN[:, c, 256 + T:256 + T + 256], [r_XNh], lambda r: GH[:, r, 0:256], [r_GH], PV_SR)
        all_xn = [r for c in range(NCH) for r in r_XN[c]] + [r_XNh]

        def load_wq(c, ws):
            for j in range(3):
                S.add("pool", lambda e, j=j: e.dma_start(out=WQ[ws][:, j], in_=w_qkv_d[j * 8 + c]), writes=[r_WQ[ws][j]], dma=f"wq{ws}", group=(j > 0))

        def qkv_chunk(c, ws):
            if c + 1 < NCH:
                load_wq(c + 1, 1 - ws)
            for tt in range(4):
                b = bank()
                mm(PSF[:, b, :], [(WQ[ws][:, 0, k, :], XN[:, k, 256 + tt * 512:256 + (tt + 1) * 512]) for k in range(NCH)],
                   [r_XN[k][tt] for k in range(NCH)] + [r_WQ[ws][0]], r_ps[b])
                S.add("act", lambda e, b=b, tt=tt: e.activation(out=Q[:, tt * 512:(tt + 1) * 512], in_=PSF[:, b, :], func=AF.Copy, scale=0.125),
                      reads=[r_ps[b]], writes=[r_Q])
            for tt in range(5):
                b = bank()
                mm(PSF[:, b, :], [(WQ[ws][:, 1, k, :], XN[:, k, tt * 512:(tt + 1) * 512]) for k in range(NCH)],
                   all_xn + [r_WQ[ws][1]], r_ps[b])
                S.add("act", lambda e, b=b, tt=tt: e.activation(out=K[:, tt * 512:(tt + 1) * 512], in_=PSF[:, b, :], func=AF.Copy),
                      reads=[r_ps[b]], writes=[r_K])
            for p4 in range(5):
                b = bank()
                for pp in range(4):
                    pb = p4 * 4 + pp
                    mm(PSF[:, b, pp * 128:(pp + 1) * 128], [(XN[:, k, pb * 128:(pb + 1) * 128], WQ[ws][:, 2, k, :]) for k in range(NCH)],
                       all_xn + [r_WQ[ws][2]], r_ps[b])
                S.add("act", lambda e, b=b, p4=p4: e.activation(out=VT[:, p4 * 4:(p4 + 1) * 4, :, 0:64],
                                                               in_=PSF[:, b, :].rearrange("p (a h d) -> p a h d", a=4, h=2), func=AF.Copy),
                      reads=[r_ps[b]], writes=[r_VT])

        def tables(h, tsl):
            S.add("sp", lambda e: e.dma_start(out=BST[:], in_=biasF_d[h]), writes=[r_BST], dma="bst")
            S.add("dve", lambda e: e.tensor_tensor(out=TABF[tsl][:], in0=BST[:], in1=MKF[:], op=ALU.add), reads=[r_BST, r_MK], writes=[r_TAB[tsl]])
            S.add("dve", lambda e: e.tensor_tensor(out=TABI[tsl][:], in0=BST[:, 192:832], in1=MKI[:], op=ALU.add), reads=[r_BST, r_MK], writes=[r_TAB[tsl]])

        wi = [0]

        def attend(c, hl, m, tsl, ot_slot, first_head):
            p0 = hl * 64
            qap = Q[p0:p0 + 64, m * 128:(m + 1) * 128]
            units = []
            top, bot = m < 2, m >= 14
            if is_p or not (top or bot):
                pbs = [pb for pb in range(m + 4, m - 1, -1)]
                if not is_p:
                    pbs = [pb for pb in pbs if 2 <= pb <= 17]
                    assert len(pbs) == 5
                flag = None
                if is_p and top:
                    flag = PV_FLAG + 0
                if is_p and bot:
                    flag = PV_FLAG + 2
                units.append((pbs, TABI[tsl][:, :], flag))
            if top:
                units.append(([5, 4, 3, 2], TABF[tsl][:, (1 + 2 * m) * 64:(1 + 2 * m) * 64 + 512], (PV_FLAG + 1) if is_p else None))
            if bot:
                e0 = 2 * m - 23
                units.append(([17, 16, 15, 14], TABF[tsl][:, e0 * 64:e0 * 64 + 512], (PV_FLAG + 3) if is_p else None))
            bo = bank()
            nmm = sum(len(u[0]) for u in units)
            done = 0
            for (pbs, tab, flag) in units:
                n = len(pbs) * 128
                sl = wi[0] % 2
                wi[0] += 1
                b = bank()
                b2 = bank() if n > 512 else None
                for s, pb in enumerate(pbs):
                    if s < 4:
                        o = PSF[:, b, s * 128:(s + 1) * 128]
                        rr = r_ps[b]
                    else:
                        o = PSF[:, b2, 0:128]
                        rr = r_ps[b2]
                    S.add("pe", lambda e, o=o, pb=pb: e.matmul(o, lhsT=K[p0:p0 + 64, pb * 128:(pb + 1) * 128], rhs=qap, start=True, stop=True),
                          reads=[r_K, r_Q], writes=[rr])
                n1 = min(n, 512)
                S.add("dve", lambda e, b=b, sl=sl, n1=n1, tab=tab: e.tensor_tensor(out=SS[sl][:, 0:n1], in0=PSF[:, b, 0:n1], in1=tab[:, 0:n1], op=ALU.add),
                      reads=[r_ps[b], r_TAB[tsl]], writes=[r_SS[sl]])
                if n > 512:
                    S.add("dve", lambda e, b2=b2, sl=sl, tab=tab: e.tensor_tensor(out=SS[sl][:, 512:640], in0=PSF[:, b2, 0:128], in1=tab[:, 512:640], op=ALU.add),
                          reads=[r_ps[b2], r_TAB[tsl], r_SS[sl]], writes=[r_SS[sl]])
                if flag is None:
                    S.add("act", lambda e, sl=sl, n=n: e.activation(out=PT[sl][:, 0:n], in_=SS[sl][:, 0:n], func=AF.Exp), reads=[r_SS[sl]], writes=[r_PT[sl]])
                else:
                    S.add("act", lambda e, sl=sl, n=n, flag=flag: e.activation(out=PT[sl][:, 0:n], in_=SS[sl][:, 0:n], func=AF.Exp, bias=pv(flag)),
                          reads=[r_SS[sl], r_PV], writes=[r_PT[sl]])
                for s, pb in enumerate(pbs):
                    S.add("pe", lambda e, s=s, pb=pb, sl=sl, i=done: e.matmul(PSF[:, bo, 0:65], lhsT=PT[sl][:, s * 128:(s + 1) * 128], rhs=VT[:, pb, hl, :],
                                                                             start=(i == 0), stop=(i == nmm - 1)),
                          reads=[r_PT[sl], r_VT], writes=[r_ps[bo]])
                    done += 1
            S.add("dve", lambda e: e.reciprocal(out=RD[:, hl:hl + 1], in_=PSF[:, bo, 64:65]), reads=[r_ps[bo]], writes=[r_RD])
            S.add("dve", lambda e: e.tensor_scalar(out=OT[ot_slot][:, p0:p0 + 64], in0=PSF[:, bo, 0:64], scalar1=RD[:, hl:hl + 1], scalar2=None, op0=ALU.mult),
                  reads=[r_ps[bo], r_RD], writes=[r_OT[ot_slot]])

        def wo_round(rnd):
            S.add("pool", lambda e: e.dma_start(out=WO[0][:], in_=w_o_d[0]), writes=[r_WO[0]], dma="wo0")
            for cp in range(NCH):
                ws = cp % 2
                if cp + 1 < NCH:
                    S.add("pool", lambda e, cp=cp, ws=ws: e.dma_start(out=WO[1 - ws][:], in_=w_o_d[cp + 1]), writes=[r_WO[1 - ws]], dma=f"wo{1 - ws}")
                for tt in range(4):
                    b = bank()
                    mm(PSF[:, b, :], [(WO[ws][:, rnd * 4 + k, :], OF[:, k, tt * 512:(tt + 1) * 512]) for k in range(4)],
                       r_OF + [r_WO[ws]], r_ps[b])
                    S.add("dve", lambda e, b=b, cp=cp, tt=tt: e.tensor_tensor(out=X[:, cp, tt * 512:(tt + 1) * 512], in0=X[:, cp, tt * 512:(tt + 1) * 512],
                                                                              in1=PSF[:, b, :], op=ALU.add),
                          reads=[r_ps[b], r_X[cp][tt]], writes=[r_X[cp][tt]])

        oti = [0]
        load_wq(0, 0)
        for rnd in range(2):
            for c4 in range(4):
                c = rnd * 4 + c4
                qkv_chunk(c, c % 2)
                tables(2 * c, 0)
                tables(2 * c + 1, 1)
                for m4 in range(4):
                    for mm_ in range(4):
                        m = m4 * 4 + mm_
                        osl = oti[0] % 2
                        oti[0] += 1
                        attend(c, 0, m, 0, osl, True)
                        attend(c, 1, m, 1, osl, False)
                        S.add("pe", lambda e, osl=osl, mm_=mm_: e.transpose(PSB[:, mm_ * 128:(mm_ + 1) * 128], OT[osl][:], IDENT[:]),
                              reads=[r_OT[osl], r_ID], writes=[r_psb])
                    S.add("act", lambda e, c4=c4, m4=m4: e.activation(out=OF[:, c4, m4 * 512:(m4 + 1) * 512], in_=PSB[:, 0:512], func=AF.Copy),
                          reads=[r_psb], writes=[r_OF[c4]])
            wo_round(rnd)

    def attention_s(is_p):
        AR.new_phase()
        XN = AR.alloc("XNA", [128, NCH, TK], BF16)
        r_XN = [[AR.res_only(f"XN{c}_{tt}") for tt in range(4)] for c in range(NCH)]
        r_XNh = AR.res_only("XNh")
        scr = norm_scratch()
        Q = AR.alloc("Q", [128, T], BF16)
        r_Q = AR.res_only("Q")
        K = AR.alloc("K", [128, TK], BF16)
        r_K = AR.res_only("K")
        VT = AR.alloc("VT", [128, 20, 2, 65], BF16)
        r_VT = AR.res_only("VT")
        of_off = AR.off
        OF = AR.alloc("OF", [128, 4, T], BF16)
        r_OF = [AR.res_only(f"OF{c}") for c in range(4)]
        TABI = [AR.alloc(f"TABI{i}", [128, 640], BF16) for i in range(2)]
        TABF = [AR.alloc(f"TABF{i}", [128, 1024], BF16) for i in range(2)]
        r_TAB = [AR.res_only(f"TAB{i}") for i in range(2)]
        if is_p:
            TABIT = [[AR.alloc(f"TABIT{i}{v}", [128, 640], BF16) for v in range(2)] for i in range(2)]
            TABFT = [[AR.alloc(f"TABFT{i}{v}", [128, 1024], BF16) for v in range(2)] for i in range(2)]
        BST = AR.alloc("BST", [128, 1024], F32)
        r_BST = AR.res_only("BST")
        MKF = AR.alloc("MKF", [128, 1024], F32)
        MKI = AR.alloc("MKI", [128, 640], F32)
        r_MK = AR.res_only("MK")
        PT = [AR.alloc(f"PT{i}", [128, 640], BF16) for i in range(4)]
        r_PT = [AR.res_only(f"PT{i}") for i in range(4)]
        OT = [AR.alloc(f"OT{i}", [128, 128], BF16) for i in range(2)]
        r_OTh = [[AR.res_only(f"OT{i}_{h}") for h in range(2)] for i in range(2)]
        RD = AR.alloc("RD", [128, 4], F32)
        r_RD = [AR.res_only(f"RD{i}") for i in range(4)]
        WQ = [AR.alloc(f"WQ{i}", [128, 3, 8, 128], BF16) for i in range(2)]
        r_WQ = [[AR.res_only(f"WQ{i}_{j}") for j in range(3)] for i in range(2)]
        WO = [AR.alloc(f"WO{i}", [128, 8, 128], BF16) for i in range(2)]
        r_WO = [AR.res_only(f"WO{i}") for i in range(2)]
        GH = nc.alloc_sbuf_tensor_at("GH_alias", [128, ncores, 512], BF16, offset=of_off) if is_p else None
        r_GH = AR.res_only("GH")
        r_c3i, r_c3o = Res("c3i"), Res("c3o")

        S.add("sp", lambda e: e.dma_start(out=MKF[:], in_=maskF_d[:, :]), writes=[r_MK], dma="mkf")
        S.add("sp", lambda e: e.dma_start(out=MKI[:], in_=maskI_d[:, :]), writes=[r_MK], dma="mki")
        norm_main(PV_G + 16, XN, r_XN, scr, col0=256)
        S.add("pool", lambda e: e.memset(XN[:, :, 0:256], 0.0), writes=[r_XNh])
        S.add("pool", lambda e: e.memset(XN[:, :, 256 + T:TK], 0.0), writes=[r_XNh])
        S.add("pool", lambda e: e.memset(VT[:, :, :, 64:65], 1.0), writes=[r_VT])
        if is_p:
            for c in range(NCH):
                S.add("sp", lambda e, c=c: e.dma_start(out=cc3_in.ap()[:, c * 512:c * 512 + 256], in_=XN[:, c, 256:512]),
                      reads=[r_XN[c][0]], writes=[r_c3i], dma="c3i", group=(c > 0))
                S.add("sp", lambda e, c=c: e.dma_start(out=cc3_in.ap()[:, c * 512 + 256:(c + 1) * 512], in_=XN[:, c, 256 + T - 256:256 + T]),
                      reads=[r_XN[c][3]], writes=[r_c3i], dma="c3i", group=True)
            import os
            if not os.environ.get("SKIPCC3"):
                S.add("pool", lambda e: e.collective_compute("AllGather", ALU.bypass, replica_groups=[list(range(ncores))],
                                                             ins=[cc3_in.ap()], outs=[cc3_out.ap()]),
                      reads=[r_c3i], writes=[r_c3o], dma="ccx3", cc=True)
            g3 = cc3_out.ap().rearrange("(r p) n -> p r n", p=128)
            for c in range(NCH):
                S.add("sp", lambda e, c=c: e.dma_start(out=GH[:], in_=g3[:, :, c * 512:(c + 1) * 512]), reads=[r_c3o], writes=[r_GH], dma="gh")
                mask_sum(XN[:, c, 0:256], [r_XNh], lambda r: GH[:, r, 256:512], [r_GH], PV_SL)
                mask_sum(XN[:, c, 256 + T:256 + T + 256], [r_XNh], lambda r: GH[:, r, 0:256], [r_GH], PV_SR)
        all_xn = [r for c in range(NCH) for r in r_XN[c]] + [r_XNh]
        gfr = list(set(r_GH.writers + r_GH.readers))
        for r in r_OF:
            r.writers = r.writers + gfr

        def load_wq(c, ws):
            for j in range(3):
                S.add("pool", lambda e, j=j: e.dma_start(out=WQ[ws][:, j], in_=w_qkv_d[j * 8 + c]), writes=[r_WQ[ws][j]], dma=f"wq{ws}", group=(j > 0))

        def qkv_chunk(c, ws):
            if c + 1 < NCH:
                load_wq(c + 1, 1 - ws)
            for tt in range(4):
                b = bank()
                mm(PSF[:, b, :], [(WQ[ws][:, 0, k, :], XN[:, k, 256 + tt * 512:256 + (tt + 1) * 512]) for k in range(NCH)],
                   [r_XN[k][tt] for k in range(NCH)] + [r_WQ[ws][0]], r_ps[b])
                S.add("act", lambda e, b=b, tt=tt: e.activation(out=Q[:, tt * 512:(tt + 1) * 512], in_=PSF[:, b, :], func=AF.Copy, scale=0.125),
                      reads=[r_ps[b]], writes=[r_Q])
            for tt in range(5):
                b = bank()
                mm(PSF[:, b, :], [(WQ[ws][:, 1, k, :], XN[:, k, tt * 512:(tt + 1) * 512]) for k in range(NCH)],
                   all_xn + [r_WQ[ws][1]], r_ps[b])
                S.add("act", lambda e, b=b, tt=tt: e.activation(out=K[:, tt * 512:(tt + 1) * 512], in_=PSF[:, b, :], func=AF.Copy),
                      reads=[r_ps[b]], writes=[r_K])
            for p4 in range(5):
                b = bank()
                for pp in range(4):
                    pb = p4 * 4 + pp
                    mm(PSF[:, b, pp * 128:(pp + 1) * 128], [(XN[:, k, pb * 128:(pb + 1) * 128], WQ[ws][:, 2, k, :]) for k in range(NCH)],
                       all_xn + [r_WQ[ws][2]], r_ps[b])
                S.add("act", lambda e, b=b, p4=p4: e.activation(out=VT[:, p4 * 4:(p4 + 1) * 4, :, 0:64],
                                                               in_=PSF[:, b, :].rearrange("p (a h d) -> p a h d", a=4, h=2), func=AF.Copy),
                      reads=[r_ps[b]], writes=[r_VT])

        def tables(h, tsl):
            S.add("sp", lambda e: e.dma_start(out=BST[:], in_=biasF_d[h]), writes=[r_BST], dma="bst")
            S.add("dve", lambda e: e.tensor_tensor(out=TABF[tsl][:], in0=BST[:], in1=MKF[:], op=ALU.add), reads=[r_BST, r_MK], writes=[r_TAB[tsl]])
            S.add("dve", lambda e: e.tensor_tensor(out=TABI[tsl][:], in0=BST[:, 192:832], in1=MKI[:], op=ALU.add), reads=[r_BST, r_MK], writes=[r_TAB[tsl]])
            if is_p:
                for v in range(2):
                    S.add("dve", lambda e, v=v: e.tensor_scalar(out=TABIT[tsl][v][:], in0=TABI[tsl][:], scalar1=pv(PV_FLAG + 2 * v), scalar2=None, op0=ALU.add),
                          reads=[r_TAB[tsl], r_PV], writes=[r_TAB[tsl]])
                    S.add("dve", lambda e, v=v: e.tensor_scalar(out=TABFT[tsl][v][:], in0=TABF[tsl][:], scalar1=pv(PV_FLAG + 2 * v + 1), scalar2=None, op0=ALU.add),
                          reads=[r_TAB[tsl], r_PV], writes=[r_TAB[tsl]])

        bank_lim[0] = 4
        r_bo = [r_ps[4], r_ps[5], r_ps[6]]
        PSFL = PSF[:].rearrange("p a n -> p (a n)")
        NPT = 4
        pt_i = [0]
        bo_i = [0]
        pair_i = [0]

        def units_of(m):
            units = []
            top, bot = m < 2, m >= 14
            if is_p or not (top or bot):
                pbs = [pb for pb in range(m + 4, m - 1, -1)]
                kind = 0
                if is_p and top:
                    kind = 2
                if is_p and bot:
                    kind = 3
                units.append((pbs, kind, 0))
            if top:
                units.append(([5, 4, 3, 2], 4 if is_p else 1, (1 + 2 * m) * 64))
            if bot:
                units.append(([17, 16, 15, 14], 5 if is_p else 1, (2 * m - 23) * 64))
            return units

        def stage_a(c, hl, m):
            p0 = hl * 64
            qap = Q[p0:p0 + 64, m * 128:(m + 1) * 128]
            out = []
            for (pbs, kind, coff) in units_of(m):
                n = len(pbs) * 128
                pr = pair_i[0] % 2
                pair_i[0] += 1
                bb = 2 * pr
                rr = [r_ps[bb], r_ps[bb + 1]] if n > 512 else [r_ps[bb]]
                tab = {0: TABI[hl], 1: TABF[hl]}[kind] if kind < 2 else {2: TABIT[hl][0], 3: TABIT[hl][1], 4: TABFT[hl][0], 5: TABFT[hl][1]}[kind]
                for (c0_, c1_) in ((0, min(n, 512)), (512, n)):
                    if c1_ <= c0_:
                        continue
                    S.add("pe", lambda e, c0_=c0_, c1_=c1_: e.matmul(PSFL[:, bb * 512 + c0_: bb * 512 + c1_], lhsT=IDENT[:], rhs=tab[:, coff + c0_: coff + c1_],
                                                                   start=True, stop=False),
                          reads=[r_ID, r_TAB[hl]], writes=rr)
                for s_, pb in enumerate(pbs):
                    o = PSFL[:, bb * 512 + s_ * 128: bb * 512 + (s_ + 1) * 128]
                    S.add("pe", lambda e, o=o, pb=pb: e.matmul(o, lhsT=K[p0:p0 + 64, pb * 128:(pb + 1) * 128], rhs=qap, start=False, stop=True),
                          reads=[r_K, r_Q], writes=rr)
                sl = pt_i[0] % NPT
                pt_i[0] += 1
                src = PSFL[:, bb * 512: bb * 512 + n]
                S.add("act", lambda e, sl=sl, n=n, src=src: e.activation(out=PT[sl][:, 0:n], in_=src, func=AF.Exp), reads=rr, writes=[r_PT[sl]])
                out.append((sl, pbs))
            return out

        def stage_b(c, hl, m, pts, ot_slot):
            p0 = hl * 64
            bs = bo_i[0] % 3
            bo_i[0] += 1
            acc = PSF[:, 4 + bs, 0:65]
            nmm = sum(len(pbs) for _, pbs in pts)
            i = 0
            for (sl, pbs) in pts:
                for s_, pb in enumerate(pbs):
                    S.add("pe", lambda e, s_=s_, pb=pb, sl=sl, i=i: e.matmul(acc, lhsT=PT[sl][:, s_ * 128:(s_ + 1) * 128], rhs=VT[:, pb, hl, :],
                                                                            start=(i == 0), stop=(i == nmm - 1)),
                          reads=[r_PT[sl], r_VT], writes=[r_bo[bs]])
                    i += 1
            S.add("dve", lambda e: e.reciprocal(out=RD[:, bs:bs + 1], in_=PSF[:, 4 + bs, 64:65]), reads=[r_bo[bs]], writes=[r_RD[bs]])
            S.add("dve", lambda e: e.tensor_scalar(out=OT[ot_slot][:, p0:p0 + 64], in0=PSF[:, 4 + bs, 0:64], scalar1=RD[:, bs:bs + 1], scalar2=None, op0=ALU.mult),
                  reads=[r_bo[bs], r_RD[bs]], writes=[r_OTh[ot_slot][hl]])

        def wo_round(rnd):
            S.add("pool", lambda e: e.dma_start(out=WO[0][:], in_=w_o_d[0]), writes=[r_WO[0]], dma="wo0")
            for cp in range(NCH):
                ws = cp % 2
                if cp + 1 < NCH:
                    S.add("pool", lambda e, cp=cp, ws=ws: e.dma_start(out=WO[1 - ws][:], in_=w_o_d[cp + 1]), writes=[r_WO[1 - ws]], dma=f"wo{1 - ws}")
                for tt in range(4):
                    b = bank()
                    mm(PSF[:, b, :], [(WO[ws][:, rnd * 4 + k, :], OF[:, k, tt * 512:(tt + 1) * 512]) for k in range(4)],
                       r_OF + [r_WO[ws]], r_ps[b])
                    S.add("dve", lambda e, b=b, cp=cp, tt=tt: e.tensor_tensor(out=X[:, cp, tt * 512:(tt + 1) * 512], in0=X[:, cp, tt * 512:(tt + 1) * 512],
                                                                              in1=PSF[:, b, :], op=ALU.add),
                          reads=[r_ps[b], r_X[cp][tt]], writes=[r_X[cp][tt]])

        load_wq(0, 0)
        for rnd in range(2):
            for c4 in range(4):
                c = rnd * 4 + c4
                qkv_chunk(c, c % 2)
                tables(2 * c, 0)
                tables(2 * c + 1, 1)
                att = [(m, hl) for m in range(16) for hl in range(2)]
                pend = stage_a(c, att[0][1], att[0][0])
                for i, (m, hl) in enumerate(att):
                    nxt = None
                    if i + 1 < len(att):
                        nxt = stage_a(c, att[i + 1][1], att[i + 1][0])
                    osl = m % 2
                    stage_b(c, hl, m, pend, osl)
                    pend = nxt
                    if hl == 1:
                        mm_ = m % 4
                        S.add("pe", lambda e, osl=osl, mm_=mm_: e.transpose(PSB[:, mm_ * 128:(mm_ + 1) * 128], OT[osl][:], IDENT[:]),
                              reads=r_OTh[osl] + [r_ID], writes=[r_psb])
                        if mm_ == 3:
                            m4 = m // 4
                            S.add("act", lambda e, c4=c4, m4=m4: e.activation(out=OF[:, c4, m4 * 512:(m4 + 1) * 512], in_=PSB[:, 0:512], func=AF.Copy),
                                  reads=[r_psb], writes=[r_OF[c4]])
            wo_round(rnd)
        bank_lim[0] = 7

    def final(dst):
        AR.new_phase()
        scr = norm_scratch()
        SQ, r_SQ, STD, r_STD = scr
        YO = [AR.alloc(f"YO{i}", [128, NCH, 512], F32) for i in range(2)]
        r_YO = [[AR.res_only(f"YO{i}_{c}") for c in range(NCH)] for i in range(2)]
        for tt in range(4):
            sl = tt % 2
            norm_tile(lambda c: X[:, c, tt * 512:(tt + 1) * 512], lambda c: [r_X[c][tt]], 512, PV_G + 32,
                      lambda c: YO[sl][:, c, :], lambda c: [r_YO[sl][c]], SQ, r_SQ, STD, r_STD)
            for c in range(NCH):
                S.add("sp", lambda e, c=c, tt=tt, sl=sl: e.dma_start(out=dst[c * 128:(c + 1) * 128, tt * 512:(tt + 1) * 512], in_=YO[sl][:, c, :]),
                      reads=[r_YO[sl][c]], dma=f"yo{sl}", group=(c > 0))

    def segment(src, dst, is_p):
        load_x(src, is_p)
        if nlayers >= 1:
            mixer_lru(is_p)
            ffn(0, is_p, 1)
        if nlayers >= 2:
            import os
            if is_p:
                attention(True)
            else:
                attention_s(False)
            ffn(1, is_p and not os.environ.get("FFN1_S"), int(os.environ.get("CCI", "2")))
        final(dst)

    if do_p:
        segment(xP, yP, True)
    if do_s:
        segment(xS, yS, False)
    S.emit()
    st.close()
    return nc


def _tile_w(w, ncol_chunks):
    K, N = w.shape
    return np.ascontiguousarray(w.reshape(K // 128, 128, N // 128, 128).transpose(2, 1, 0, 3))


def _pcol(v):
    return np.ascontiguousarray(v.reshape(-1, 128).T)


def _attn_tables(rpb):
    H = rpb.shape[0]
    biasF = np.zeros((H, 128, 16, 64), np.float32)
    maskF = np.full((128, 16, 64), NEG, np.float32)
    maskI = np.full((128, 10, 64), NEG, np.float32)
    cidx = np.arange(64)
    cs = np.clip(cidx - 8, 0, 48)
    for krl in range(2):
        for kc in range(64):
            p = krl * 64 + kc
            colok = (cs <= kc) & (kc < cs + 16)
            dc = kc - cidx + 15
            for e in range(16):
                dr = 14 - e + krl
                if dr < 0 or dr > 14:
                    continue
                ok = colok & (dc >= 0) & (dc <= 30)
                cc = cidx[ok]
                biasF[:, p, e, cc] = rpb[:, dr, dc[ok]]
                maskF[p, e, cc] = 0.0
                if 3 <= e < 13 and 3 <= dr <= 10:
                    maskI[p, e - 3, cc] = 0.0
    return biasF.reshape(H, 128, 1024), maskF.reshape(128, 1024), maskI.reshape(128, 640)


_NC_CACHE = {}


def _get_nc(**kw):
    key = tuple(sorted(kw.items()))
    if key not in _NC_CACHE:
        _NC_CACHE[key] = build(**kw)
    return _NC_CACHE[key]


def make_in_maps(inp, ncores=NCORES):
    f = lambda a: np.asarray(a, dtype=np.float32)
    x_prompt, x_sample = f(inp["x_prompt"]), f(inp["x_sample"])
    shared = {}
    w_in = f(inp["lru_w_in"])[0]
    shared["w_in_t"] = _tile_w(w_in, 16)
    gw = f(inp["lru_gate_w"])[0]
    shared["gw_t"] = np.ascontiguousarray(gw.transpose(2, 3, 0, 1, 4).reshape(8, 128, 4, 128))
    shared["w_out_t"] = _tile_w(f(inp["lru_w_out"])[0], 8)
    shared["w_up_t"] = np.stack([_tile_w(f(inp["ffn_w_up"])[l], 44) for l in range(2)])
    wd = f(inp["ffn_w_down"])
    shared["w_dn_t"] = np.ascontiguousarray(wd.reshape(2, NFC, 128, D).transpose(0, 2, 1, 3))
    shared["w_qkv_t"] = _tile_w(f(inp["attn_w_qkv"])[0], 24)
    shared["w_o_t"] = _tile_w(f(inp["attn_w_o"])[0], 8)
    bF, mF, mI = _attn_tables(f(inp["attn_rpb"])[0])
    shared["biasF"], shared["maskF"], shared["maskI"] = bF, mF, mI
    shared["ident"] = np.eye(128, dtype=np.float32)

    pvb = np.zeros((128, NPV), np.float32)
    nm, nf, nfin = f(inp["norm_mix"]), f(inp["norm_ffn"]), f(inp["norm_final"])
    pvb[:, 0:8] = _pcol(nm[0]); pvb[:, 8:16] = _pcol(nf[0]); pvb[:, 16:24] = _pcol(nm[1])
    pvb[:, 24:32] = _pcol(nf[1]); pvb[:, 32:40] = _pcol(nfin)
    lcw = f(inp["lru_conv_w"])[0]
    for k in range(4):
        pvb[:, PV_LCW + 8 * k:PV_LCW + 8 * k + 8] = _pcol(lcw[k])
    pvb[:, PV_LCB:PV_LCB + 8] = _pcol(f(inp["lru_conv_b"])[0])
    lgb = f(inp["lru_gate_b"])[0]
    for d in range(2):
        for g in range(2):
            pvb[:, PV_LGB + (d * 2 + g) * 8:PV_LGB + (d * 2 + g) * 8 + 8] = lgb[d, g].T
    lam = f(inp["lru_lambda"])[0]
    for d in range(2):
        pvb[:, PV_LAM + d * 8:PV_LAM + d * 8 + 8] = _pcol(lam[d])
    fcw, fcbias = f(inp["ffn_conv_w"]), f(inp["ffn_conv_b"])
    for l in range(2):
        base = PV_FC + 176 * l
        for k in range(3):
            pvb[:, base + 44 * k:base + 44 * k + 44] = _pcol(fcw[l, k])
        pvb[:, base + 132:base + 176] = _pcol(fcbias[l])
    pvb[:, NPV - 1] = EPS
    pvb[:, NPV - 2] = 1.0
    maps = []
    for c in range(ncores):
        m = dict(shared)
        pvc = pvb.copy()
        pvc[:, PV_FLAG + 0] = NEG if c == 0 else 0.0
        pvc[:, PV_FLAG + 1] = 0.0 if c == 0 else NEG
        pvc[:, PV_FLAG + 2] = NEG if c == ncores - 1 else 0.0
        pvc[:, PV_FLAG + 3] = 0.0 if c == ncores - 1 else NEG
        for r in range(ncores):
            pvc[:, PV_MF + r] = 1.0 if r < c else 0.0
            pvc[:, PV_MB + r] = 1.0 if r > c else 0.0
            pvc[:, PV_SL + r] = 1.0 if r == c - 1 else 0.0
            pvc[:, PV_SR + r] = 1.0 if r == c + 1 else 0.0
        m["pvec"] = pvc
        m["xS"] = np.ascontiguousarray(x_sample[c].T)
        s, e = c * T, (c + 1) * T
        m["xP"] = np.ascontiguousarray(x_prompt[0, s:e].T)
        halo = np.zeros((4, D), np.float32)
        if c > 0:
            halo[0:2] = x_prompt[0, s - 2:s]
        if c < ncores - 1:
            halo[2:3] = x_prompt[0, e:e + 1]
        m["xPh"] = np.ascontiguousarray(halo.T)
        maps.append(m)
    return maps


def kernel(**inputs):
    nc = _get_nc()
    maps = make_in_maps(inputs)
    res = run_bass_kernel_spmd(nc, maps, core_ids=list(range(NCORES)))
    yp = np.concatenate([res.results[c]["yP"].T for c in range(NCORES)], axis=0)[None]
    ys = np.stack([res.results[c]["yS"].T for c in range(NCORES)], axis=0)
    return (np.ascontiguousarray(yp, dtype=np.float32), np.ascontiguousarray(ys, dtype=np.float32))
```
